# Optimizing a Trainium2 kernel written in Bass

```python
import math
import jax
import jax.numpy as jnp
from jax import lax
import numpy as np

D_MODEL = 1024
BATCH = 2
SEQ = 8192
DEPTH = 4

CHUNK = 64
SSM_WIDTH = D_MODEL // 2
SSM_GROUP = 16
SSM_GROUPS = SSM_WIDTH // SSM_GROUP
SSM_STATE = 64
ATTN_WIDTH = D_MODEL // 2
N_HEADS = 8
HEAD_DIM = ATTN_WIDTH // N_HEADS
Q_BLOCK = 128
EPS = 1e-6
DT_MIN = 0.001
DT_MAX = 0.1
IN_SIZES = (SSM_WIDTH, SSM_WIDTH, ATTN_WIDTH, ATTN_WIDTH, ATTN_WIDTH, ATTN_WIDTH, D_MODEL, D_MODEL)
IN_COLS = sum(IN_SIZES)
IN_SPLITS = tuple(int(s) for s in np.cumsum(IN_SIZES)[:-1])

kernel_name = "hybrid_s5_stickbreaking_gated_block"


def rms_norm(x, g):
    xf = x.astype(jnp.float32)
    y = xf * lax.rsqrt(jnp.mean(xf * xf, axis=-1, keepdims=True) + EPS)
    return (y * g.astype(jnp.float32)).astype(x.dtype)


def s5_branch(u, a_re, a_im, log_dt, b_re, b_im, c_re, c_im, d_skip, w_glu, b_glu):
    bsz, seqlen, _ = u.shape
    f32 = jnp.float32
    ug = u.reshape(bsz, seqlen, SSM_GROUPS, SSM_GROUP).astype(f32)
    a_re = a_re.astype(f32)
    a_im = a_im.astype(f32)
    dt = jnp.exp(log_dt.astype(f32))[:, None]
    mag = jnp.exp(a_re * dt)
    abar_re = mag * jnp.cos(a_im * dt)
    abar_im = mag * jnp.sin(a_im * dt)
    nr = abar_re - 1.0
    ni = abar_im
    den = a_re * a_re + a_im * a_im
    f_re = (nr * a_re + ni * a_im) / den
    f_im = (ni * a_re - nr * a_im) / den
    b_re = b_re.astype(f32)
    b_im = b_im.astype(f32)
    bb_re = f_re[..., None] * b_re - f_im[..., None] * b_im
    bb_im = f_re[..., None] * b_im + f_im[..., None] * b_re
    bu_re = jnp.einsum('blgh,gph->blgp', ug, bb_re)
    bu_im = jnp.einsum('blgh,gph->blgp', ug, bb_im)
    ar_t = jnp.broadcast_to(abar_re, bu_re.shape)
    ai_t = jnp.broadcast_to(abar_im, bu_re.shape)

    def combine(e1, e2):
        a1r, a1i, b1r, b1i = e1
        a2r, a2i, b2r, b2i = e2
        return (a1r * a2r - a1i * a2i,
                a1r * a2i + a1i * a2r,
                a2r * b1r - a2i * b1i + b2r,
                a2r * b1i + a2i * b1r + b2i)

    _, _, xr, xi = lax.associative_scan(combine, (ar_t, ai_t, bu_re, bu_im), axis=1)
    y = (jnp.einsum('blgp,ghp->blgh', xr, c_re.astype(f32))
         - jnp.einsum('blgp,ghp->blgh', xi, c_im.astype(f32)))
    y = y + d_skip.astype(f32).reshape(SSM_GROUPS, SSM_GROUP) * ug
    y = jax.nn.gelu(y.reshape(bsz, seqlen, SSM_WIDTH)).astype(u.dtype)
    gl = y @ w_glu + b_glu
    ga, gb = jnp.split(gl, 2, axis=-1)
    return ga * jax.nn.sigmoid(gb)


def stick_breaking_attention(q, k, v):
    seqlen = q.shape[1]
    scale = HEAD_DIM ** -0.5
    outs = []
    for i in range(seqlen // Q_BLOCK):
        q0 = i * Q_BLOCK
        kend = q0 + Q_BLOCK
        qb = q[:, q0:kend]
        kb = k[:, :kend]
        vb = v[:, :kend]
        z = jnp.einsum('bqhd,bkhd->bhqk', qb, kb).astype(jnp.float32) * scale
        qpos = q0 + jnp.arange(Q_BLOCK)[:, None]
        kpos = jnp.arange(kend)[None, :]
        mask = kpos < qpos
        log_beta = jax.nn.log_sigmoid(z)
        log_1m = jnp.where(mask, log_beta - z, 0.0)
        after = lax.cumsum(log_1m, axis=3, reverse=True) - log_1m
        w = jnp.where(mask, jnp.exp(log_beta + after), 0.0)
        outs.append(jnp.einsum('bhqk,bkhd->bqhd', w.astype(v.dtype), vb))
    return jnp.concatenate(outs, axis=1)


def setup_inputs(seed: int = 0) -> dict:
    key = jax.random.key(seed)
    ks = jax.random.split(key, 20)
    f32 = jnp.float32
    nrm = lambda k, shape, s: jax.random.normal(k, shape, f32) * s
    x = jax.random.normal(ks[0], (BATCH, SEQ, D_MODEL), f32)
    pre_norm_g = 1.0 + nrm(ks[1], (DEPTH, D_MODEL), 0.02)
    post_norm_g = 1.0 + nrm(ks[2], (DEPTH, D_MODEL), 0.02)
    w_in = nrm(ks[3], (DEPTH, D_MODEL, IN_COLS), D_MODEL ** -0.5)
    ssm_a_re = -0.5 + nrm(ks[4], (DEPTH, SSM_GROUPS, SSM_STATE), 0.01)
    ssm_a_im = (math.pi * jnp.arange(SSM_STATE, dtype=f32))[None, None, :] + nrm(ks[5], (DEPTH, SSM_GROUPS, SSM_STATE), 0.01)
    ssm_log_dt = jax.random.uniform(ks[6], (DEPTH, SSM_GROUPS), f32, math.log(DT_MIN), math.log(DT_MAX))
    ssm_b_re = nrm(ks[7], (DEPTH, SSM_GROUPS, SSM_STATE, SSM_GROUP), (2 * SSM_GROUP) ** -0.5)
    ssm_b_im = nrm(ks[8], (DEPTH, SSM_GROUPS, SSM_STATE, SSM_GROUP), (2 * SSM_GROUP) ** -0.5)
    ssm_c_re = nrm(ks[9], (DEPTH, SSM_GROUPS, SSM_GROUP, SSM_STATE), (2 * SSM_STATE) ** -0.5)
    ssm_c_im = nrm(ks[10], (DEPTH, SSM_GROUPS, SSM_GROUP, SSM_STATE), (2 * SSM_STATE) ** -0.5)
    ssm_d = nrm(ks[11], (DEPTH, SSM_WIDTH), 1.0)
    w_glu = nrm(ks[12], (DEPTH, SSM_WIDTH, 2 * SSM_WIDTH), SSM_WIDTH ** -0.5)
    b_glu = nrm(ks[13], (DEPTH, 2 * SSM_WIDTH), 0.01)
    w_branch_ssm = nrm(ks[14], (DEPTH, SSM_WIDTH, D_MODEL), SSM_WIDTH ** -0.5)
    w_branch_attn = nrm(ks[15], (DEPTH, ATTN_WIDTH, D_MODEL), ATTN_WIDTH ** -0.5)
    w_out = nrm(ks[16], (DEPTH, D_MODEL, D_MODEL), D_MODEL ** -0.5)
    return {"x": x, "pre_norm_g": pre_norm_g, "post_norm_g": post_norm_g, "w_in": w_in,
            "ssm_a_re": ssm_a_re, "ssm_a_im": ssm_a_im, "ssm_log_dt": ssm_log_dt,
            "ssm_b_re": ssm_b_re, "ssm_b_im": ssm_b_im, "ssm_c_re": ssm_c_re, "ssm_c_im": ssm_c_im,
            "ssm_d": ssm_d, "w_glu": w_glu, "b_glu": b_glu,
            "w_branch_ssm": w_branch_ssm, "w_branch_attn": w_branch_attn, "w_out": w_out}


def reference(x, pre_norm_g, post_norm_g, w_in, ssm_a_re, ssm_a_im, ssm_log_dt,
              ssm_b_re, ssm_b_im, ssm_c_re, ssm_c_im, ssm_d, w_glu, b_glu,
              w_branch_ssm, w_branch_attn, w_out):
    bsz, seqlen, _ = x.shape
    for l in range(DEPTH):
        h = rms_norm(x, pre_norm_g[l])
        proj = h @ w_in[l]
        u, z_ssm, q, k, v, z_attn, g_ssm, g_attn = jnp.split(proj, IN_SPLITS, axis=-1)
        y_s = s5_branch(u, ssm_a_re[l], ssm_a_im[l], ssm_log_dt[l], ssm_b_re[l], ssm_b_im[l],
                        ssm_c_re[l], ssm_c_im[l], ssm_d[l], w_glu[l], b_glu[l])
        y_s = y_s * jax.nn.silu(z_ssm)
        hs = (bsz, seqlen, N_HEADS, HEAD_DIM)
        y_a = stick_breaking_attention(q.reshape(hs), k.reshape(hs), v.reshape(hs))
        y_a = y_a.reshape(bsz, seqlen, ATTN_WIDTH) * jax.nn.silu(z_attn)
        merged = (jax.nn.sigmoid(g_ssm) * (y_s @ w_branch_ssm[l])
                  + jax.nn.sigmoid(g_attn) * (y_a @ w_branch_attn[l]))
        out = merged @ w_out[l]
        x = x + rms_norm(out, post_norm_g[l])
    return x
```

```python
import math
import numpy as np
import ml_dtypes
import concourse.bass as bass
import concourse.mybir as mybir
from concourse.bass_utils import run_bass_kernel_spmd

F32 = mybir.dt.float32
BF16 = mybir.dt.bfloat16
I32 = mybir.dt.int32
AF = mybir.ActivationFunctionType
ALU = mybir.AluOpType
NPBF = ml_dtypes.bfloat16

D_MODEL = 1024
BATCH = 2
SEQ = 8192
DEPTH = 4
NCORES = 8
EPS = 1e-6


class Sched:
    ENGS = ("pe", "act", "dve", "pool", "sp")

    def __init__(self, nc):
        self.nc = nc
        self.lists = {e: [] for e in self.ENGS}
        self.count = {}
        self.seen = {e: {} for e in self.ENGS}
        self.lastw = {}
        self.readers = {}
        self.semkeys = list(self.ENGS)
        self.final = []

    def _deps(self, eng, reads, writes):
        deps = []
        for b in list(reads) + list(writes):
            t = self.lastw.get(b)
            if t is not None:
                deps.append(t)
        for b in writes:
            deps.extend(self.readers.get(b, ()))
        waits = {}
        for (k, v) in deps:
            if eng == "pe" and k == "pe":
                continue
            if self.seen[eng].get(k, 0) >= v:
                continue
            if waits.get(k, 0) < v:
                waits[k] = v
        for k, v in waits.items():
            self.seen[eng][k] = v
        return list(waits.items())

    def _commit(self, ticket, reads, writes):
        for b in writes:
            self.lastw[b] = ticket
            self.readers[b] = []
        for b in reads:
            if b not in writes:
                self.readers.setdefault(b, []).append(ticket)

    def op(self, eng, fn, reads=(), writes=()):
        waits = self._deps(eng, reads, writes)
        self.count[eng] = self.count.get(eng, 0) + 1
        ticket = (eng, self.count[eng])
        self.lists[eng].append((waits, fn, (eng, 1)))
        self._commit(ticket, reads, writes)
        return ticket

    def dma(self, eng, out, in_, slot, reads=(), writes=()):
        key = "dma_" + slot
        if key not in self.semkeys:
            self.semkeys.append(key)
        waits = self._deps(eng, reads, writes)
        self.count[key] = self.count.get(key, 0) + 16
        ticket = (key, self.count[key])
        self.lists[eng].append((waits, lambda e: e.dma_start(out=out, in_=in_), (key, 16)))
        self._commit(ticket, reads, writes)
        return ticket

    def wait_all(self, eng, tickets):
        self.final.append((eng, list(tickets)))

    def emit(self):
        nc = self.nc
        import contextlib
        with contextlib.ExitStack() as st:
            sems = {k: st.enter_context(nc.semaphore("s_" + k)) for k in self.semkeys}
            block = st.enter_context(nc.Block())
            engmap = {"pe": block.tensor, "act": block.scalar, "dve": block.vector,
                      "pool": block.gpsimd, "sp": block.sync}
            finals = {}
            for eng, ts in self.final:
                finals.setdefault(eng, []).extend(ts)

            def mk(ename):
                lst = self.lists[ename]
                fin = finals.get(ename, [])

                def body(e):
                    for waits, fn, (k, inc) in lst:
                        for wk, wv in waits:
                            e.wait_ge(sems[wk], wv)
                        ins = fn(e)
                        ins.then_inc(sems[k], inc)
                    for (wk, wv) in fin:
                        e.wait_ge(sems[wk], wv)
                return body

            for ename in self.ENGS:
                if self.lists[ename] or finals.get(ename):
                    engmap[ename](mk(ename))


def _ap(t):
    return t if isinstance(t, bass.AP) else t[:]


def build_attn(L):
    nc = bass.Bass("TRN2", target_bir_lowering=False)
    NQ = L // 128
    qT_d = nc.dram_tensor("qT", [2, 128, L], BF16, kind="ExternalInput").ap()
    kT_d = nc.dram_tensor("kTr", [128, L], BF16, kind="ExternalInput").ap()
    v_d = nc.dram_tensor("vr", [L, 128], BF16, kind="ExternalInput").ap()
    mask_d = nc.dram_tensor("dmask", [128, 128], BF16, kind="ExternalInput").ap()
    id_d = nc.dram_tensor("ident", [128, 128], BF16, kind="ExternalInput").ap()
    y_d = nc.dram_tensor("yT", [128, L], F32, kind="ExternalOutput").ap()
    NB = 3
    import contextlib
    with contextlib.ExitStack() as st:
        sb = lambda name, shape, dt: st.enter_context(nc.sbuf_tensor(name, shape, dt))
        ps = lambda name, shape, dt: st.enter_context(nc.psum_tensor(name, shape, dt))
        qT = sb("qT_s", [128, 2, L], BF16)
        kT = sb("kT_s", [128, L], BF16)
        vr = sb("vr_s", [128, NQ, 128], BF16)
        yT = sb("yT_s", [128, L], F32)
        mask = sb("mask_s", [128, 128], BF16)
        ident = sb("ident_s", [128, 128], BF16)
        zeros = sb("zeros_s", [128, 512], F32)
        r = [sb(f"r{i}", [128, 512], F32) for i in range(NB)]
        Pb = [sb(f"Pb{i}", [128, 520], F32) for i in range(NB)]
        w = [sb(f"w{i}", [128, 512], BF16) for i in range(NB)]
        wT = [sb(f"wT{i}", [128, 512], BF16) for i in range(2)]
        z = [ps(f"z{i}", [128, 512], F32) for i in range(2)]
        wTp = [ps(f"wTp{i}", [128, 1024], BF16) for i in range(2)]
        acc = [ps(f"acc{i}", [128, 512], F32) for i in range(2)]

        S = Sched(nc)
        S.dma("sp", qT[:, 0, :], qT_d[0], "q0", writes=["qT"])
        S.dma("sp", qT[:, 1, :], qT_d[1], "q1", writes=["qT"])
        S.dma("sp", kT[:], kT_d, "k", writes=["kT"])
        S.dma("sp", vr[:], v_d.rearrange("(n p) d -> p n d", p=128), "v", writes=["vr"])
        S.dma("sp", mask[:], mask_d, "mask", writes=["mask"])
        S.dma("sp", ident[:], id_d, "ident", writes=["ident"])
        S.op("pool", lambda e: e.memset(zeros[:], 0.0), writes=["zeros"])

        items = []
        for h in range(2):
            for qi in range(NQ):
                nk = 128 * (qi + 1)
                s0 = 128 * (NQ - 1 - qi)
                nch = (nk + 511) // 512
                for c in range(nch):
                    n = min(512, nk - 512 * c)
                    items.append(dict(h=h, qi=qi, c=c, n=n, s0=s0 + 512 * c, last=(c == nch - 1)))
        NI = len(items)

        def stage_qk(i):
            it = items[i]
            zb = z[i % 2]
            n, h, qi, s0 = it["n"], it["h"], it["qi"], it["s0"]
            diag = it["c"] == 0
            S.op("pe", lambda e: e.matmul(zb[:, 0:n], lhsT=qT[:, h, qi * 128:(qi + 1) * 128],
                                          rhs=kT[:, s0:s0 + n], start=True, stop=not diag),
                 reads=["qT", "kT"], writes=[f"z{i % 2}"])
            if diag:
                S.op("pe", lambda e: e.matmul(zb[:, 0:128], lhsT=ident[:], rhs=mask[:], start=False, stop=True),
                     reads=["ident", "mask"], writes=[f"z{i % 2}"])
            rb = r[i % NB]
            S.op("act", lambda e: e.activation(out=rb[:, 0:n], in_=zb[:, 0:n], func=AF.Sigmoid, scale=-1.0),
                 reads=[f"z{i % 2}"], writes=[f"r{i % NB}"])
            pb = Pb[i % NB]
            if it["c"] == 0:
                S.op("dve", lambda e: e.memset(pb[:, 0:1], 1.0), writes=[f"Pb{i % NB}"])
            else:
                pprev = Pb[(i - 1) % NB]
                nprev = items[i - 1]["n"]
                S.op("dve", lambda e: e.tensor_copy(out=pb[:, 0:1], in_=pprev[:, nprev:nprev + 1]),
                     reads=[f"Pb{(i - 1) % NB}"], writes=[f"Pb{i % NB}"])
            S.op("dve", lambda e: e.tensor_tensor_scan(out=pb[:, 1:n + 1], data0=rb[:, 0:n], data1=zeros[:, 0:n],
                                                       initial=pb[:, 0:1], op0=ALU.mult, op1=ALU.add),
                 reads=[f"r{i % NB}", "zeros", f"Pb{i % NB}"], writes=[f"Pb{i % NB}"])
            wb = w[i % NB]
            S.op("dve", lambda e: e.tensor_tensor(out=wb[:, 0:n], in0=pb[:, 0:n], in1=pb[:, 1:n + 1], op=ALU.subtract),
                 reads=[f"Pb{i % NB}"], writes=[f"w{i % NB}"])

        def stage_tr(i):
            it = items[i]
            n = it["n"]
            wb = w[i % NB]
            tp = wTp[i % 2]
            for sub in range(n // 128):
                S.op("pe", lambda e, sub=sub: e.transpose(out=tp[:, sub * 128:(sub + 1) * 128],
                                                          in_=wb[:, sub * 128:(sub + 1) * 128], identity=ident[:]),
                     reads=[f"w{i % NB}", "ident"], writes=[f"wTp{i % 2}"])
            wtb = wT[i % 2]
            S.op("act", lambda e: e.copy(out=wtb[:, 0:n], in_=tp[:, 0:n]),
                 reads=[f"wTp{i % 2}"], writes=[f"wT{i % 2}"])

        def stage_av(i):
            it = items[i]
            n, h, qi, s0 = it["n"], it["h"], it["qi"], it["s0"]
            wtb = wT[i % 2]
            ab = acc[qi % 2]
            nsub = n // 128
            for sub in range(nsub):
                S.op("pe", lambda e, sub=sub: e.matmul(ab[:, 0:128], lhsT=vr[:, s0 // 128 + sub, :],
                                                       rhs=wtb[:, sub * 128:(sub + 1) * 128],
                                                       start=(it["c"] == 0 and sub == 0),
                                                       stop=(it["last"] and sub == nsub - 1)),
                     reads=[f"wT{i % 2}", "vr"], writes=[f"acc{qi % 2}"])
            if it["last"]:
                S.op("act", lambda e: e.copy(out=yT[64 * h:64 * h + 64, qi * 128:(qi + 1) * 128],
                                             in_=ab[64 * h:64 * h + 64, 0:128]),
                     reads=[f"acc{qi % 2}"], writes=["yT"])

        for s in range(NI + 2):
            if s < NI:
                stage_qk(s)
            if 0 <= s - 1 < NI:
                stage_tr(s - 1)
            if 0 <= s - 2 < NI:
                stage_av(s - 2)
        t = S.dma("sp", y_d, yT[:], "yout", reads=["yT"])
        S.wait_all("sp", [t])
        S.emit()
    return nc


def attn_consts():
    a = np.arange(128)
    dmask = (-30000.0 * (a[:, None] + a[None, :] <= 127)).astype(np.float32).astype(NPBF)
    ident = np.eye(128, dtype=np.float32).astype(NPBF)
    return dict(dmask=dmask, ident=ident)


TWO_PI = 2.0 * math.pi
CW1 = 6.28125
CW2 = TWO_PI - CW1


LAST_TMP = None


class Tmp:
    def __init__(self, nc, st):
        global LAST_TMP
        self.nc, self.st, self.n = nc, st, 0
        self.names = []
        LAST_TMP = self

    def scratch(self, shape, dt, name):
        key = (name, tuple(shape), str(dt))
        if not hasattr(self, "_scr"):
            self._scr = {}
        if key not in self._scr:
            self._scr[key] = self.tile(shape, dt, name)
        return self._scr[key]

    def tile(self, shape, dt=F32, name=None):
        self.n += 1
        nm = f"{name or 't'}_{self.n}"
        t = self.st.enter_context(self.nc.sbuf_tensor(nm, shape, dt))
        self.names.append(nm)
        return t, nm


def emit_sincos(S, T, ang, ang_n, n, eng="dve"):
    outs = []
    for shift in (0.0, math.pi / 2):
        kf, kf_n = T.scratch([128, n], F32, "kf")
        ki, ki_n = T.scratch([128, n], I32, "ki")
        rr, rr_n = T.scratch([128, n], F32, "rr")
        fx, fx_n = T.scratch([128, n], F32, "fx")
        res, res_n = T.tile([128, n], F32, "sc")
        S.op(eng, lambda e, kf=kf, shift=shift: e.tensor_scalar(out=kf[:], in0=ang[:], scalar1=shift, scalar2=1.0 / TWO_PI,
                                                   op0=ALU.add, op1=ALU.mult), reads=[ang_n], writes=[kf_n])
        S.op(eng, lambda e, kf=kf, ki=ki: e.tensor_copy(out=ki[:], in_=kf[:]), reads=[kf_n], writes=[ki_n])
        S.op(eng, lambda e, kf=kf, ki=ki: e.tensor_copy(out=kf[:], in_=ki[:]), reads=[ki_n], writes=[kf_n])
        S.op(eng, lambda e, rr=rr, shift=shift: e.tensor_scalar(out=rr[:], in0=ang[:], scalar1=shift, scalar2=None, op0=ALU.add),
             reads=[ang_n], writes=[rr_n])
        S.op(eng, lambda e, rr=rr, kf=kf: e.scalar_tensor_tensor(out=rr[:], in0=kf[:], scalar=-CW1, in1=rr[:],
                                                                 op0=ALU.mult, op1=ALU.add), reads=[kf_n, rr_n], writes=[rr_n])
        S.op(eng, lambda e, rr=rr, kf=kf: e.scalar_tensor_tensor(out=rr[:], in0=kf[:], scalar=-CW2, in1=rr[:],
                                                                 op0=ALU.mult, op1=ALU.add), reads=[kf_n, rr_n], writes=[rr_n])
        S.op(eng, lambda e, rr=rr, fx=fx: e.tensor_scalar(out=fx[:], in0=rr[:], scalar1=math.pi, scalar2=-TWO_PI,
                                                          op0=ALU.is_gt, op1=ALU.mult), reads=[rr_n], writes=[fx_n])
        S.op(eng, lambda e, rr=rr, fx=fx: e.tensor_tensor(out=rr[:], in0=rr[:], in1=fx[:], op=ALU.add),
             reads=[rr_n, fx_n], writes=[rr_n])
        S.op(eng, lambda e, rr=rr, fx=fx: e.tensor_scalar(out=fx[:], in0=rr[:], scalar1=-math.pi, scalar2=TWO_PI,
                                                          op0=ALU.is_lt, op1=ALU.mult), reads=[rr_n], writes=[fx_n])
        S.op(eng, lambda e, rr=rr, fx=fx: e.tensor_tensor(out=rr[:], in0=rr[:], in1=fx[:], op=ALU.add),
             reads=[rr_n, fx_n], writes=[rr_n])
        S.op(eng, lambda e, rr=rr: e.tensor_scalar(out=rr[:], in0=rr[:], scalar1=-3.14159, scalar2=3.14159,
                                                   op0=ALU.max, op1=ALU.min), reads=[rr_n], writes=[rr_n])
        S.op("act", lambda e, rr=rr, res=res: e.activation(out=res[:], in_=rr[:], func=AF.Sin), reads=[rr_n], writes=[res_n])
        outs += [res, res_n]
    return outs


def emit_ssm_params(S, T, are, aim, ldt, names, n):
    are_n, aim_n, ldt_n = names
    tt = lambda nm: T.tile([128, n], F32, nm)
    dt_, dt_n = tt("dt")
    S.op("act", lambda e: e.activation(out=dt_[:], in_=ldt[:], func=AF.Exp), reads=[ldt_n], writes=[dt_n])
    ard, ard_n = tt("ard")
    th, th_n = tt("theta")
    S.op("dve", lambda e: e.tensor_tensor(out=ard[:], in0=are[:], in1=dt_[:], op=ALU.mult), reads=[are_n, dt_n], writes=[ard_n])
    S.op("dve", lambda e: e.tensor_tensor(out=th[:], in0=aim[:], in1=dt_[:], op=ALU.mult), reads=[aim_n, dt_n], writes=[th_n])
    rho, rho_n = tt("rho")
    S.op("act", lambda e: e.activation(out=rho[:], in_=ard[:], func=AF.Exp), reads=[ard_n], writes=[rho_n])
    sn, sn_n, cs, cs_n = emit_sincos(S, T, th, th_n, n)
    abr, abr_n = tt("abr")
    abi, abi_n = tt("abi")
    S.op("dve", lambda e: e.tensor_tensor(out=abr[:], in0=rho[:], in1=cs[:], op=ALU.mult), reads=[rho_n, cs_n], writes=[abr_n])
    S.op("dve", lambda e: e.tensor_tensor(out=abi[:], in0=rho[:], in1=sn[:], op=ALU.mult), reads=[rho_n, sn_n], writes=[abi_n])
    S.op("dve", lambda e: e.tensor_scalar(out=abr[:], in0=abr[:], scalar1=-1.0, scalar2=None, op0=ALU.add), reads=[abr_n], writes=[abr_n])
    den, den_n = tt("den")
    t2, t2_n = tt("t2")
    S.op("dve", lambda e: e.tensor_tensor(out=den[:], in0=are[:], in1=are[:], op=ALU.mult), reads=[are_n], writes=[den_n])
    S.op("dve", lambda e: e.tensor_tensor(out=t2[:], in0=aim[:], in1=aim[:], op=ALU.mult), reads=[aim_n], writes=[t2_n])
    S.op("dve", lambda e: e.tensor_tensor(out=den[:], in0=den[:], in1=t2[:], op=ALU.add), reads=[den_n, t2_n], writes=[den_n])
    S.op("dve", lambda e: e.reciprocal(out=den[:], in_=den[:]), reads=[den_n], writes=[den_n])
    fre, fre_n = tt("fre")
    fim, fim_n = tt("fim")
    S.op("dve", lambda e: e.tensor_tensor(out=fre[:], in0=abr[:], in1=are[:], op=ALU.mult), reads=[abr_n, are_n], writes=[fre_n])
    S.op("dve", lambda e: e.tensor_tensor(out=t2[:], in0=abi[:], in1=aim[:], op=ALU.mult), reads=[abi_n, aim_n], writes=[t2_n])
    S.op("dve", lambda e: e.tensor_tensor(out=fre[:], in0=fre[:], in1=t2[:], op=ALU.add), reads=[fre_n, t2_n], writes=[fre_n])
    S.op("dve", lambda e: e.tensor_tensor(out=fre[:], in0=fre[:], in1=den[:], op=ALU.mult), reads=[fre_n, den_n], writes=[fre_n])
    S.op("dve", lambda e: e.tensor_tensor(out=fim[:], in0=abi[:], in1=are[:], op=ALU.mult), reads=[abi_n, are_n], writes=[fim_n])
    S.op("dve", lambda e: e.tensor_tensor(out=t2[:], in0=abr[:], in1=aim[:], op=ALU.mult), reads=[abr_n, aim_n], writes=[t2_n])
    S.op("dve", lambda e: e.tensor_tensor(out=fim[:], in0=fim[:], in1=t2[:], op=ALU.subtract), reads=[fim_n, t2_n], writes=[fim_n])
    S.op("dve", lambda e: e.tensor_tensor(out=fim[:], in0=fim[:], in1=den[:], op=ALU.mult), reads=[fim_n, den_n], writes=[fim_n])
    return (rho, rho_n), (th, th_n), (fre, fre_n), (fim, fim_n)


def build_ssm(L):
    nc = bass.Bass("TRN2", target_bir_lowering=False)
    NCH = L // 512
    u_d = nc.dram_tensor("uT", [128, L], F32, kind="ExternalInput").ap()
    pA_d = nc.dram_tensor("pA", [128, 12], F32, kind="ExternalInput").ap()
    pB_d = nc.dram_tensor("pB", [128, 192], F32, kind="ExternalInput").ap()
    bT_d = nc.dram_tensor("bT", [128, 128], F32, kind="ExternalInput").ap()
    sel_d = nc.dram_tensor("sel", [128, 8], F32, kind="ExternalInput").ap()
    cL_d = nc.dram_tensor("cL", [128, 2 * 4 * 128], F32, kind="ExternalInput").ap()
    dsk_d = nc.dram_tensor("dsk", [128, 1], F32, kind="ExternalInput").ap()
    ramp_d = nc.dram_tensor("ramp", [128, 512], F32, kind="ExternalInput").ap()
    y_d = nc.dram_tensor("yT", [128, L], F32, kind="ExternalOutput").ap()
    import contextlib
    with contextlib.ExitStack() as st:
        S = Sched(nc)
        T = Tmp(nc, st)
        ps = lambda name, shape, dt: st.enter_context(nc.psum_tensor(name, shape, dt))
        uc = [T.tile([128, 512], F32, "uc") for _ in range(2)]
        ubc = [T.tile([128, 512], BF16, "ubc") for _ in range(2)]
        yoc = [T.tile([128, 512], F32, "yoc") for _ in range(2)]
        pA, pA_n = T.tile([128, 12], F32, "pA")
        pB, pB_n = T.tile([128, 192], F32, "pB")
        bT, bT_n = T.tile([128, 128], F32, "bT")
        sel, sel_n = T.tile([128, 8], F32, "sel")
        cL, cL_n = T.tile([128, 1024], F32, "cL")
        cLb, cLb_n = T.tile([128, 1024], BF16, "cLb")
        dsk, dsk_n = T.tile([128, 1], F32, "dsk")
        ramp, ramp_n = T.tile([128, 512], F32, "ramp")
        S.dma("sp", pA[:], pA_d, "pA", writes=[pA_n])
        S.dma("sp", pB[:], pB_d, "pB", writes=[pB_n])
        S.dma("sp", bT[:], bT_d, "bT", writes=[bT_n])
        S.dma("sp", sel[:], sel_d, "sel", writes=[sel_n])
        S.dma("sp", cL[:], cL_d, "cL", writes=[cL_n])
        S.dma("sp", dsk[:], dsk_d, "dsk", writes=[dsk_n])
        S.dma("sp", ramp[:], ramp_d, "ramp", writes=[ramp_n])
        S.op("dve", lambda e: e.tensor_copy(out=cLb[:, 0:512], in_=cL[:, 0:512]), reads=[cL_n], writes=[cLb_n])
        S.op("dve", lambda e: e.tensor_scalar(out=cLb[:, 512:1024], in0=cL[:, 512:1024], scalar1=-1.0, scalar2=None, op0=ALU.mult),
             reads=[cL_n], writes=[cLb_n])

        def cols(t, a, b):
            return t[:, a:b]
        areA, areA_n = T.tile([128, 4], F32, "areA"); aimA, aimA_n = T.tile([128, 4], F32, "aimA"); ldtA, ldtA_n = T.tile([128, 4], F32, "ldtA")
        for dst, dn, k in ((areA, areA_n, 0), (aimA, aimA_n, 1), (ldtA, ldtA_n, 2)):
            S.op("dve", lambda e, dst=dst, k=k: e.tensor_copy(out=dst[:], in_=pA[:, 4 * k:4 * k + 4]), reads=[pA_n], writes=[dn])
        (rhoA, rhoA_n), (thA, thA_n), _, _ = emit_ssm_params(S, T, areA, aimA, ldtA, (areA_n, aimA_n, ldtA_n), 4)
        areB, areB_n = T.tile([128, 64], F32, "areB"); aimB, aimB_n = T.tile([128, 64], F32, "aimB"); ldtB, ldtB_n = T.tile([128, 64], F32, "ldtB")
        for dst, dn, k in ((areB, areB_n, 0), (aimB, aimB_n, 1), (ldtB, ldtB_n, 2)):
            S.op("dve", lambda e, dst=dst, k=k: e.tensor_copy(out=dst[:], in_=pB[:, 64 * k:64 * k + 64]), reads=[pB_n], writes=[dn])
        _, _, (freB, freB_n), (fimB, fimB_n) = emit_ssm_params(S, T, areB, aimB, ldtB, (areB_n, aimB_n, ldtB_n), 64)
        bbr, bbr_n = T.tile([128, 64], F32, "bbr"); bbi, bbi_n = T.tile([128, 64], F32, "bbi"); tq, tq_n = T.tile([128, 64], F32, "tq")
        S.op("dve", lambda e: e.tensor_tensor(out=bbr[:], in0=freB[:], in1=bT[:, 0:64], op=ALU.mult), reads=[freB_n, bT_n], writes=[bbr_n])
        S.op("dve", lambda e: e.tensor_tensor(out=tq[:], in0=fimB[:], in1=bT[:, 64:128], op=ALU.mult), reads=[fimB_n, bT_n], writes=[tq_n])
        S.op("dve", lambda e: e.tensor_tensor(out=bbr[:], in0=bbr[:], in1=tq[:], op=ALU.subtract), reads=[bbr_n, tq_n], writes=[bbr_n])
        S.op("dve", lambda e: e.tensor_tensor(out=bbi[:], in0=freB[:], in1=bT[:, 64:128], op=ALU.mult), reads=[freB_n, bT_n], writes=[bbi_n])
        S.op("dve", lambda e: e.tensor_tensor(out=tq[:], in0=fimB[:], in1=bT[:, 0:64], op=ALU.mult), reads=[fimB_n, bT_n], writes=[tq_n])
        S.op("dve", lambda e: e.tensor_tensor(out=bbi[:], in0=bbi[:], in1=tq[:], op=ALU.add), reads=[bbi_n, tq_n], writes=[bbi_n])
        LB, LB_n = T.tile([128, 1024], BF16, "LB")
        for ri, src, src_n in ((0, bbr, bbr_n), (1, bbi, bbi_n)):
            for i in range(4):
                for gg in range(2):
                    o = ri * 512 + i * 128 + gg * 64
                    S.op("dve", lambda e, o=o, src=src, i=i, gg=gg: e.tensor_scalar(
                        out=LB[:, o:o + 64], in0=src[:], scalar1=sel[:, 2 * i + gg:2 * i + gg + 1], scalar2=None, op0=ALU.mult),
                        reads=[src_n, sel_n], writes=[LB_n])
        ctab, stab, rhob = [], [], []
        for i in range(4):
            ang, ang_n = T.scratch([128, 512], F32, "ang")
            S.op("dve", lambda e, ang=ang, i=i: e.tensor_scalar(out=ang[:], in0=ramp[:], scalar1=thA[:, i:i + 1], scalar2=None, op0=ALU.mult),
                 reads=[ramp_n, thA_n], writes=[ang_n])
            sn, sn_n, cs, cs_n = emit_sincos(S, T, ang, ang_n, 512)
            ctab.append((cs, cs_n)); stab.append((sn, sn_n))
            rb, rb_n = T.tile([128, 512], F32, "rhob")
            S.op("dve", lambda e, rb=rb, i=i: e.tensor_scalar(out=rb[:], in0=ramp[:], scalar1=0.0, scalar2=rhoA[:, i:i + 1],
                                                               op0=ALU.mult, op1=ALU.add), reads=[ramp_n, rhoA_n], writes=[rb_n])
            rhob.append((rb, rb_n))

        Br = [ps(f"Br{k}", [128, 512], F32) for k in range(2)]
        Bi = [ps(f"Bi{k}", [128, 512], F32) for k in range(2)]
        Yp = [ps(f"Yp{k}", [128, 512], F32) for k in range(2)]
        NW = 2
        wk = {}
        for nm in ("t1", "t2", "mtr", "t3", "t4", "mti", "mr", "mi", "t5", "t6", "t7", "t8"):
            wk[nm] = [T.tile([128, 512], F32, nm) for _ in range(NW)]
        xr = [[T.tile([128, 512], F32, f"xr{i}") for _ in range(2)] for i in range(4)]
        xi = [[T.tile([128, 512], F32, f"xi{i}") for _ in range(2)] for i in range(4)]
        xrb = [T.tile([128, 512], BF16, "xrb") for _ in range(NW)]
        xib = [T.tile([128, 512], BF16, "xib") for _ in range(NW)]
        gl = {nm: [T.tile([128, 512], F32, nm) for _ in range(2)] for nm in ("yy", "y2", "inn", "sg")}
        TT = lambda eng, o, a, b, op: S.op(eng, lambda e: e.tensor_tensor(out=o[0][:], in0=a[0][:], in1=b[0][:], op=op),
                                            reads=[a[1], b[1]], writes=[o[1]])
        unit = 0
        outs = []
        for ck in range(NCH):
            sl = slice(ck * 512, (ck + 1) * 512)
            UC, UB, YO = uc[ck % 2], ubc[ck % 2], yoc[ck % 2]
            S.dma("sp", UC[0][:], u_d[:, sl], f"u{ck % 2}", writes=[UC[1]])
            S.op("pool", lambda e, UC=UC, UB=UB: e.tensor_copy(out=UB[0][:], in_=UC[0][:]), reads=[UC[1]], writes=[UB[1]])
            for i in range(4):
                k2 = unit % 2
                kw = unit % NW
                brp, bip = Br[k2], Bi[k2]
                S.op("pe", lambda e, brp=brp, i=i, UB=UB: e.matmul(brp[:], lhsT=LB[:, i * 128:(i + 1) * 128], rhs=UB[0][:], start=True, stop=True),
                     reads=[LB_n, UB[1]], writes=[f"Br{k2}"])
                S.op("pe", lambda e, bip=bip, i=i, UB=UB: e.matmul(bip[:], lhsT=LB[:, 512 + i * 128:512 + (i + 1) * 128], rhs=UB[0][:], start=True, stop=True),
                     reads=[LB_n, UB[1]], writes=[f"Bi{k2}"])
                c_, s_ = ctab[i], stab[i]
                BR = (brp, f"Br{k2}"); BI = (bip, f"Bi{k2}")
                g = lambda nm: wk[nm][kw]
                TT("dve", g("t1"), c_, BR, ALU.mult)
                TT("dve", g("t2"), s_, BI, ALU.mult)
                TT("dve", g("mtr"), g("t1"), g("t2"), ALU.add)
                TT("dve", g("t3"), c_, BI, ALU.mult)
                TT("dve", g("t4"), s_, BR, ALU.mult)
                TT("dve", g("mti"), g("t3"), g("t4"), ALU.subtract)
                for (mt, mo, xs) in ((g("mtr"), g("mr"), xr), (g("mti"), g("mi"), xi)):
                    if ck == 0:
                        S.op("dve", lambda e, mt=mt, mo=mo, i=i: e.tensor_tensor_scan(out=mo[0][:], data0=rhob[i][0][:], data1=mt[0][:], initial=0.0,
                                                                                    op0=ALU.mult, op1=ALU.add),
                             reads=[rhob[i][1], mt[1]], writes=[mo[1]])
                    else:
                        prev = xs[i][(ck - 1) % 2]
                        S.op("dve", lambda e, mt=mt, mo=mo, i=i, prev=prev: e.tensor_tensor_scan(
                            out=mo[0][:], data0=rhob[i][0][:], data1=mt[0][:], initial=prev[0][:, 511:512], op0=ALU.mult, op1=ALU.add),
                            reads=[rhob[i][1], mt[1], prev[1]], writes=[mo[1]])
                XR = xr[i][ck % 2]; XI = xi[i][ck % 2]
                TT("pool", g("t5"), c_, g("mr"), ALU.mult)
                TT("pool", g("t6"), s_, g("mi"), ALU.mult)
                TT("pool", XR, g("t5"), g("t6"), ALU.subtract)
                TT("pool", g("t7"), c_, g("mi"), ALU.mult)
                TT("pool", g("t8"), s_, g("mr"), ALU.mult)
                TT("pool", XI, g("t7"), g("t8"), ALU.add)
                XRB = xrb[kw]; XIB = xib[kw]
                S.op("act", lambda e, XRB=XRB, XR=XR: e.copy(out=XRB[0][:], in_=XR[0][:]), reads=[XR[1]], writes=[XRB[1]])
                S.op("act", lambda e, XIB=XIB, XI=XI: e.copy(out=XIB[0][:], in_=XI[0][:]), reads=[XI[1]], writes=[XIB[1]])
                yp = Yp[ck % 2]
                S.op("pe", lambda e, yp=yp, i=i, XRB=XRB: e.matmul(yp[:], lhsT=cLb[:, i * 128:(i + 1) * 128], rhs=XRB[0][:], start=(i == 0), stop=False),
                     reads=[cLb_n, XRB[1]], writes=[f"Yp{ck % 2}"])
                S.op("pe", lambda e, yp=yp, i=i, XIB=XIB: e.matmul(yp[:], lhsT=cLb[:, 512 + i * 128:512 + (i + 1) * 128], rhs=XIB[0][:], start=False, stop=(i == 3)),
                     reads=[cLb_n, XIB[1]], writes=[f"Yp{ck % 2}"])
                unit += 1
            yp = Yp[ck % 2]
            YY, Y2, INN, SG = (gl[nm][ck % 2] for nm in ("yy", "y2", "inn", "sg"))
            S.op("dve", lambda e, yp=yp, YY=YY, UC=UC: e.scalar_tensor_tensor(out=YY[0][:], in0=UC[0][:], scalar=dsk[:, 0:1], in1=yp[:],
                                                                             op0=ALU.mult, op1=ALU.add),
                 reads=[UC[1], dsk_n, f"Yp{ck % 2}"], writes=[YY[1]])
            TT("pool", Y2, YY, YY, ALU.mult)
            S.op("pool", lambda e, Y2=Y2: e.tensor_scalar(out=Y2[0][:], in0=Y2[0][:], scalar1=0.044715 * 1.5957691216057308, scalar2=1.5957691216057308,
                                                          op0=ALU.mult, op1=ALU.add), reads=[Y2[1]], writes=[Y2[1]])
            TT("pool", INN, Y2, YY, ALU.mult)
            S.op("act", lambda e, SG=SG, INN=INN: e.activation(out=SG[0][:], in_=INN[0][:], func=AF.Sigmoid), reads=[INN[1]], writes=[SG[1]])
            S.op("pool", lambda e, SG=SG, YY=YY, YO=YO: e.tensor_tensor(out=YO[0][:], in0=YY[0][:], in1=SG[0][:], op=ALU.mult),
                 reads=[SG[1], YY[1]], writes=[YO[1]])
            outs.append(S.dma("sp", y_d[:, sl], YO[0][:], f"yo{ck % 2}", reads=[YO[1]]))
        S.wait_all("sp", outs)
        S.emit()
    return nc


def ssm_layout(l, j, a_re, a_im, log_dt, b_re, b_im, c_re, c_im, d):
    g0 = 8 * j
    are = a_re[l, g0:g0 + 8]; aim = a_im[l, g0:g0 + 8]; ldt = log_dt[l, g0:g0 + 8]
    stA = lambda m: np.ascontiguousarray(m.reshape(4, 2, 64).transpose(1, 2, 0).reshape(128, 4))
    pA = np.concatenate([stA(are), stA(aim), stA(np.repeat(ldt[:, None], 64, axis=1))], axis=1).astype(np.float32)
    chB = lambda m: np.ascontiguousarray(np.repeat(m[:, None, :], 16, axis=1).reshape(128, 64))
    pB = np.concatenate([chB(are), chB(aim), chB(np.repeat(ldt[:, None], 64, axis=1))], axis=1).astype(np.float32)
    bt = lambda m: np.ascontiguousarray(m[l, g0:g0 + 8].transpose(0, 2, 1).reshape(128, 64))
    bT = np.concatenate([bt(b_re), bt(b_im)], axis=1).astype(np.float32)
    sel = np.zeros((128, 8), np.float32)
    cL = np.zeros((128, 2, 4, 128), np.float32)
    for g in range(8):
        i, gg = g // 2, g % 2
        sel[16 * g:16 * g + 16, 2 * i + gg] = 1.0
        cL[gg * 64:(gg + 1) * 64, 0, i, 16 * g:16 * g + 16] = c_re[l, g0 + g].T
        cL[gg * 64:(gg + 1) * 64, 1, i, 16 * g:16 * g + 16] = c_im[l, g0 + g].T
    dsk = np.ascontiguousarray(d[l, 128 * j:128 * j + 128].reshape(128, 1)).astype(np.float32)
    ramp = np.ascontiguousarray(np.broadcast_to(np.arange(1, 513, dtype=np.float32), (128, 512)))
    return dict(pA=pA, pB=pB, bT=bT, sel=sel, cL=cL.reshape(128, 1024), dsk=dsk, ramp=ramp)


def emit_rms_rstd(S, T, ss_n, ss, rstd, rstd_n):
    S.op("dve", lambda e: e.tensor_scalar(out=rstd[:], in0=ss[:], scalar1=1.0 / D_MODEL, scalar2=EPS, op0=ALU.mult, op1=ALU.add),
         reads=[ss_n], writes=[rstd_n])
    S.op("act", lambda e: e.activation(out=rstd[:], in_=rstd[:], func=AF.Sqrt), reads=[rstd_n], writes=[rstd_n])
    S.op("dve", lambda e: e.reciprocal(out=rstd[:], in_=rstd[:]), reads=[rstd_n], writes=[rstd_n])


def build_pre(NT):
    nc = bass.Bass("TRN2", target_bir_lowering=False)
    NBLK = NT // 512
    x_d = nc.dram_tensor("x", [NT, 1024], F32, kind="ExternalInput").ap()
    g_d = nc.dram_tensor("gpre", [128, 1024], F32, kind="ExternalInput").ap()
    w_d = nc.dram_tensor("w_in", [1024, 5120], F32, kind="ExternalInput").ap()
    id_d = nc.dram_tensor("ident", [128, 128], BF16, kind="ExternalInput").ap()
    uT_d = nc.dram_tensor("uT", [512, NT], F32, kind="ExternalOutput").ap()
    qT_d = nc.dram_tensor("qT", [512, NT], BF16, kind="ExternalOutput").ap()
    kT_d = nc.dram_tensor("kT", [512, NT], BF16, kind="ExternalOutput").ap()
    v_d = nc.dram_tensor("v", [NT, 512], BF16, kind="ExternalOutput").ap()
    szs_d = nc.dram_tensor("szs", [512, NT], F32, kind="ExternalOutput").ap()
    sza_d = nc.dram_tensor("sza", [512, NT], F32, kind="ExternalOutput").ap()
    sgs_d = nc.dram_tensor("sgs", [1024, NT], F32, kind="ExternalOutput").ap()
    sga_d = nc.dram_tensor("sga", [1024, NT], F32, kind="ExternalOutput").ap()
    import contextlib
    with contextlib.ExitStack() as st:
        S = Sched(nc)
        T = Tmp(nc, st)
        ps = lambda name, shape, dt: st.enter_context(nc.psum_tensor(name, shape, dt))
        Wb, Wb_n = T.tile([128, 8, 5120], BF16, "Wb")
        gb, gb_n = T.tile([128, 1024], F32, "gb")
        ident, ident_n = T.tile([128, 128], BF16, "ident")
        S.dma("sp", gb[:], g_d, "g", writes=[gb_n])
        S.dma("sp", ident[:], id_d, "id", writes=[ident_n])
        wv = w_d.rearrange("(kt p) c -> p kt c", p=128)
        for kt in range(8):
            S.dma("pool", Wb[:, kt, :], wv[:, kt, :], f"w{kt}", writes=[f"Wb{kt}"])
        xv = x_d.rearrange("(n p) d -> p n d", p=128)
        xt = [T.tile([128, 4, 1024], F32, "xt") for _ in range(2)]
        hb = [T.tile([128, 1024], BF16, "hb") for _ in range(2)]
        sq = T.tile([128, 1024], F32, "sq")
        hT = [T.tile([128, 8, 512], BF16, "hT") for _ in range(2)]
        ss = [T.tile([128, 1], F32, "ss") for _ in range(2)]
        rstd = [T.tile([128, 1], F32, "rstd") for _ in range(2)]
        tp = [ps(f"tp{k}", [128, 1024], BF16) for k in range(2)]
        mm = [ps(f"mm{k}", [128, 512], F32) for k in range(4)]
        stf = [T.tile([128, 512], F32, "stf") for _ in range(4)]
        stb = [T.tile([128, 512], BF16, "stb") for _ in range(4)]
        outs = []
        nf = nb = nmm = ntt = 0
        for blk in range(NBLK):
            X = xt[blk % 2]
            S.dma("sp", X[0][:], xv[:, blk * 4:(blk + 1) * 4, :], f"x{blk % 2}", writes=[X[1]])
            HT = hT[blk % 2]
            hbs = []
            for tt in range(4):
                k = ntt % 2; ntt += 1
                SS, RS, HB = ss[k], rstd[k], hb[k]
                S.op("act", lambda e, X=X, tt=tt, SS=SS: e.activation(out=sq[0][:], in_=X[0][:, tt, :], func=AF.Square, accum_out=SS[0][:]),
                     reads=[X[1]], writes=[sq[1], SS[1]])
                emit_rms_rstd(S, T, SS[1], SS[0], RS[0], RS[1])
                S.op("dve", lambda e, X=X, tt=tt, RS=RS, HB=HB: e.scalar_tensor_tensor(out=HB[0][:], in0=X[0][:, tt, :], scalar=RS[0][:, 0:1], in1=gb[:],
                                                                                     op0=ALU.mult, op1=ALU.mult),
                     reads=[X[1], RS[1], gb_n], writes=[HB[1]])
                for half in range(2):
                    TP = tp[half]
                    for q4 in range(4):
                        kt = half * 4 + q4
                        S.op("pe", lambda e, TP=TP, q4=q4, kt=kt, HB=HB: e.transpose(out=TP[:, q4 * 128:(q4 + 1) * 128], in_=HB[0][:, kt * 128:(kt + 1) * 128], identity=ident[:]),
                             reads=[HB[1], ident_n], writes=[f"tp{half}"])
                    S.op("act", lambda e, TP=TP, half=half, tt=tt, HT=HT: e.copy(out=HT[0][:, half * 4:(half + 1) * 4, tt * 128:(tt + 1) * 128],
                                                                               in_=TP[:, 0:512].rearrange("p (k t) -> p k t", k=4)),
                         reads=[f"tp{half}"], writes=[HT[1]])
            tsl = slice(blk * 512, (blk + 1) * 512)
            for ct in range(40):
                c0 = ct * 128
                if 2048 <= c0 < 2560:
                    continue
                M = mm[nmm % 4]; mname = f"mm{nmm % 4}"; nmm += 1
                for kt in range(8):
                    S.op("pe", lambda e, M=M, kt=kt, c0=c0, HT=HT: e.matmul(M[:], lhsT=Wb[:, kt, c0:c0 + 128], rhs=HT[0][:, kt, :], start=(kt == 0), stop=(kt == 7)),
                         reads=[f"Wb{kt}", HT[1]], writes=[mname])
                if c0 < 512:
                    kind, dst, r0 = "copy", uT_d, c0
                elif c0 < 1024:
                    kind, dst, r0 = "silu", szs_d, c0 - 512
                elif c0 < 1536:
                    kind, dst, r0 = "q", qT_d, c0 - 1024
                elif c0 < 2048:
                    kind, dst, r0 = "k", kT_d, c0 - 1536
                elif c0 < 3072:
                    kind, dst, r0 = "silu", sza_d, c0 - 2560
                elif c0 < 4096:
                    kind, dst, r0 = "sig", sgs_d, c0 - 3072
                else:
                    kind, dst, r0 = "sig", sga_d, c0 - 4096
                if kind in ("q", "k"):
                    O = stb[nb % 4]; slot = f"ob{nb % 4}"; nb += 1
                else:
                    O = stf[nf % 4]; slot = f"of{nf % 4}"; nf += 1
                if kind == "copy" or kind == "k":
                    S.op("act", lambda e, O=O, M=M: e.copy(out=O[0][:], in_=M[:]), reads=[mname], writes=[O[1]])
                elif kind == "q":
                    S.op("act", lambda e, O=O, M=M: e.mul(out=O[0][:], in_=M[:], mul=0.125), reads=[mname], writes=[O[1]])
                elif kind == "silu":
                    S.op("act", lambda e, O=O, M=M: e.activation(out=O[0][:], in_=M[:], func=AF.Silu), reads=[mname], writes=[O[1]])
                else:
                    S.op("act", lambda e, O=O, M=M: e.activation(out=O[0][:], in_=M[:], func=AF.Sigmoid), reads=[mname], writes=[O[1]])
                outs.append(S.dma("sp", dst[r0:r0 + 128, tsl], O[0][:], slot, reads=[O[1]]))
            for tt in range(4):
                M = mm[nmm % 4]; mname = f"mm{nmm % 4}"; nmm += 1
                for kt in range(8):
                    S.op("pe", lambda e, M=M, kt=kt, tt=tt, HT=HT: e.matmul(M[:], lhsT=HT[0][:, kt, tt * 128:(tt + 1) * 128], rhs=Wb[:, kt, 2048:2560], start=(kt == 0), stop=(kt == 7)),
                         reads=[f"Wb{kt}", HT[1]], writes=[mname])
                O = stb[nb % 4]; slot = f"ob{nb % 4}"; nb += 1
                S.op("act", lambda e, O=O, M=M: e.copy(out=O[0][:], in_=M[:]), reads=[mname], writes=[O[1]])
                r0 = blk * 512 + tt * 128
                outs.append(S.dma("sp", v_d[r0:r0 + 128, :], O[0][:], slot, reads=[O[1]]))
        S.wait_all("sp", outs)
        S.emit()
    return nc


def build_post(NT):
    nc = bass.Bass("TRN2", target_bir_lowering=False)
    NBLK = NT // 512
    din = lambda n, s, d=F32: nc.dram_tensor(n, s, d, kind="ExternalInput").ap()
    x_d = din("x", [NT, 1024]); g_d = din("gpost", [128, 1024])
    ys_d = din("ysT", [512, NT]); ya_d = din("yaT", [512, NT])
    szs_d = din("szs", [512, NT]); sza_d = din("sza", [512, NT]); sgs_d = din("sgs", [1024, NT]); sga_d = din("sga", [1024, NT])
    wglu_d = din("w_glu", [512, 1024]); bglu_d = din("b_glu", [128, 8]); wbs_d = din("w_bs", [512, 1024]); wba_d = din("w_ba", [512, 1024])
    wout_d = din("w_out", [1024, 1024])
    xo_d = nc.dram_tensor("xo", [NT, 1024], F32, kind="ExternalOutput").ap()
    import contextlib
    with contextlib.ExitStack() as st:
        S = Sched(nc)
        T = Tmp(nc, st)
        ps = lambda name, shape, dt: st.enter_context(nc.psum_tensor(name, shape, dt))
        Wg, Wg_n = T.tile([128, 4, 1024], BF16, "Wg"); Wbs, Wbs_n = T.tile([128, 4, 1024], BF16, "Wbs"); Wba, Wba_n = T.tile([128, 4, 1024], BF16, "Wba")
        Wo, Wo_n = T.tile([128, 8, 1024], BF16, "Wo")
        bg, bg_n = T.tile([128, 8], F32, "bg"); gp, gp_n = T.tile([128, 1024], F32, "gp")
        S.dma("sp", bg[:], bglu_d, "bg", writes=[bg_n]); S.dma("sp", gp[:], g_d, "gp", writes=[gp_n])
        S.dma("pool", Wg[:], wglu_d.rearrange("(kt p) c -> p kt c", p=128), "wg", writes=[Wg_n])
        S.dma("pool", Wbs[:], wbs_d.rearrange("(kt p) c -> p kt c", p=128), "wbs", writes=[Wbs_n])
        S.dma("pool", Wba[:], wba_d.rearrange("(kt p) c -> p kt c", p=128), "wba", writes=[Wba_n])
        S.dma("pool", Wo[:], wout_d.rearrange("(kt p) c -> p kt c", p=128), "wo", writes=[Wo_n])
        xv = x_d.rearrange("(n p) d -> p n d", p=128); xov = xo_d.rearrange("(n p) d -> p n d", p=128)
        fm = lambda d, rows: d.rearrange("(kt p) t -> p kt t", p=128)
        xt = [T.tile([128, 4, 1024], F32, "xt") for _ in range(2)]
        ysb = [T.tile([128, 4, 512], BF16, "ysb") for _ in range(2)]
        yaf = [T.tile([128, 4, 512], F32, "yaf") for _ in range(1)]
        zs = [T.tile([128, 4, 512], F32, "zs") for _ in range(1)]
        za = [T.tile([128, 4, 512], F32, "za") for _ in range(1)]
        gs = [T.tile([128, 8, 512], F32, "gs") for _ in range(1)]
        ga_ = [T.tile([128, 8, 512], F32, "ga") for _ in range(1)]
        yab = T.tile([128, 4, 512], BF16, "yab")
        ysg = T.tile([128, 4, 512], BF16, "ysg")
        mg = T.tile([128, 8, 512], BF16, "mg")
        sgb = [T.tile([128, 512], F32, "sgb") for _ in range(2)]
        tg = [T.tile([128, 512], F32, "tg") for _ in range(2)]
        m1 = [T.tile([128, 512], F32, "m1") for _ in range(2)]
        m2 = [T.tile([128, 512], F32, "m2") for _ in range(2)]
        sq = T.tile([128, 512], F32, "sq")
        ssA = [T.tile([128, 1], F32, "ssA") for _ in range(2)]
        ssB = [T.tile([128, 1], F32, "ssB") for _ in range(2)]
        rstd = [T.tile([128, 1], F32, "rstd") for _ in range(2)]
        tn = [T.tile([128, 1024], F32, "tn") for _ in range(2)]
        pA = [ps(f"pA{k}", [128, 512], F32) for k in range(2)]
        pB = [ps(f"pB{k}", [128, 512], F32) for k in range(2)]
        pO = [ps(f"pO{k}", [128, 512], F32) for k in range(4)]
        outs = []
        ng = nbr = nto = 0
        for blk in range(NBLK):
            b2 = blk % 2
            tsl = slice(blk * 512, (blk + 1) * 512)
            X, YS, YA, ZS, ZA, GS, GA = xt[b2], ysb[b2], yaf[0], zs[0], za[0], gs[0], ga_[0]
            S.dma("sp", X[0][:], xv[:, blk * 4:(blk + 1) * 4, :], f"x{b2}", writes=[X[1]])
            S.dma("pool", YS[0][:], fm(ys_d, 512)[:, :, tsl], f"ys{b2}", writes=[YS[1]])
            S.dma("sp", YA[0][:], fm(ya_d, 512)[:, :, tsl], "ya0", writes=[YA[1]])
            S.dma("sp", ZS[0][:], fm(szs_d, 512)[:, :, tsl], "zs0", writes=[ZS[1]])
            S.dma("sp", ZA[0][:], fm(sza_d, 512)[:, :, tsl], "za0", writes=[ZA[1]])
            S.dma("sp", GS[0][:], fm(sgs_d, 1024)[:, :, tsl], "gs0", writes=[GS[1]])
            S.dma("sp", GA[0][:], fm(sga_d, 1024)[:, :, tsl], "ga0", writes=[GA[1]])
            S.op("pool", lambda e, YA=YA, ZA=ZA: e.tensor_tensor(out=yab[0][:], in0=YA[0][:], in1=ZA[0][:], op=ALU.mult),
                 reads=[YA[1], ZA[1]], writes=[yab[1]])
            for c in range(4):
                k = ng % 2; ng += 1
                A, B = pA[k], pB[k]
                for kt in range(4):
                    S.op("pe", lambda e, A=A, kt=kt, c=c, YS=YS: e.matmul(A[:], lhsT=Wg[:, kt, c * 128:(c + 1) * 128], rhs=YS[0][:, kt, :], start=(kt == 0), stop=(kt == 3)),
                         reads=[Wg_n, YS[1]], writes=[f"pA{k}"])
                for kt in range(4):
                    S.op("pe", lambda e, B=B, kt=kt, c=c, YS=YS: e.matmul(B[:], lhsT=Wg[:, kt, 512 + c * 128:512 + (c + 1) * 128], rhs=YS[0][:, kt, :], start=(kt == 0), stop=(kt == 3)),
                         reads=[Wg_n, YS[1]], writes=[f"pB{k}"])
                SG, TG = sgb[k], tg[k]
                S.op("act", lambda e, SG=SG, B=B, c=c: e.activation(out=SG[0][:], in_=B[:], func=AF.Sigmoid, bias=bg[:, 4 + c:5 + c]),
                     reads=[f"pB{k}", bg_n], writes=[SG[1]])
                S.op("dve", lambda e, TG=TG, A=A, SG=SG, c=c: e.scalar_tensor_tensor(out=TG[0][:], in0=A[:], scalar=bg[:, c:c + 1], in1=SG[0][:], op0=ALU.add, op1=ALU.mult),
                     reads=[f"pA{k}", bg_n, SG[1]], writes=[TG[1]])
                S.op("pool", lambda e, TG=TG, ZS=ZS, c=c: e.tensor_tensor(out=ysg[0][:, c, :], in0=TG[0][:], in1=ZS[0][:, c, :], op=ALU.mult),
                     reads=[TG[1], ZS[1]], writes=[ysg[1]])
            for c in range(8):
                k = nbr % 2; nbr += 1
                A, B = pA[k], pB[k]
                for kt in range(4):
                    S.op("pe", lambda e, A=A, kt=kt, c=c: e.matmul(A[:], lhsT=Wbs[:, kt, c * 128:(c + 1) * 128], rhs=ysg[0][:, kt, :], start=(kt == 0), stop=(kt == 3)),
                         reads=[Wbs_n, ysg[1]], writes=[f"pA{k}"])
                for kt in range(4):
                    S.op("pe", lambda e, B=B, kt=kt, c=c: e.matmul(B[:], lhsT=Wba[:, kt, c * 128:(c + 1) * 128], rhs=yab[0][:, kt, :], start=(kt == 0), stop=(kt == 3)),
                         reads=[Wba_n, yab[1]], writes=[f"pB{k}"])
                M1, M2 = m1[k], m2[k]
                S.op("dve", lambda e, M1=M1, A=A, GS=GS, c=c: e.tensor_tensor(out=M1[0][:], in0=A[:], in1=GS[0][:, c, :], op=ALU.mult),
                     reads=[f"pA{k}", GS[1]], writes=[M1[1]])
                S.op("dve", lambda e, M2=M2, B=B, GA=GA, c=c: e.tensor_tensor(out=M2[0][:], in0=B[:], in1=GA[0][:, c, :], op=ALU.mult),
                     reads=[f"pB{k}", GA[1]], writes=[M2[1]])
                S.op("pool", lambda e, M1=M1, M2=M2, c=c: e.tensor_tensor(out=mg[0][:, c, :], in0=M1[0][:], in1=M2[0][:], op=ALU.add),
                     reads=[M1[1], M2[1]], writes=[mg[1]])
            for tt in range(4):
                k = nto % 2; nto += 1
                O0, O1 = pO[2 * k], pO[2 * k + 1]
                for half, O in ((0, O0), (1, O1)):
                    for kt in range(8):
                        S.op("pe", lambda e, O=O, kt=kt, tt=tt, half=half: e.matmul(O[:], lhsT=mg[0][:, kt, tt * 128:(tt + 1) * 128], rhs=Wo[:, kt, half * 512:(half + 1) * 512],
                                                                                    start=(kt == 0), stop=(kt == 7)),
                             reads=[mg[1], Wo_n], writes=[f"pO{2 * k + half}"])
                SA, SB, RS, TN = ssA[k], ssB[k], rstd[k], tn[k]
                S.op("act", lambda e, O0=O0, SA=SA: e.activation(out=sq[0][:], in_=O0[:], func=AF.Square, accum_out=SA[0][:]),
                     reads=[f"pO{2 * k}"], writes=[sq[1], SA[1]])
                S.op("act", lambda e, O1=O1, SB=SB: e.activation(out=sq[0][:], in_=O1[:], func=AF.Square, accum_out=SB[0][:]),
                     reads=[f"pO{2 * k + 1}"], writes=[sq[1], SB[1]])
                S.op("dve", lambda e, SA=SA, SB=SB: e.tensor_tensor(out=SA[0][:], in0=SA[0][:], in1=SB[0][:], op=ALU.add), reads=[SA[1], SB[1]], writes=[SA[1]])
                emit_rms_rstd(S, T, SA[1], SA[0], RS[0], RS[1])
                for half, O in ((0, O0), (1, O1)):
                    S.op("dve", lambda e, O=O, half=half, RS=RS, TN=TN: e.scalar_tensor_tensor(out=TN[0][:, half * 512:(half + 1) * 512], in0=O[:], scalar=RS[0][:, 0:1],
                                                                                            in1=gp[:, half * 512:(half + 1) * 512], op0=ALU.mult, op1=ALU.mult),
                         reads=[f"pO{2 * k + half}", RS[1], gp_n], writes=[TN[1]])
                S.op("pool", lambda e, X=X, tt=tt, TN=TN: e.tensor_tensor(out=X[0][:, tt, :], in0=X[0][:, tt, :], in1=TN[0][:], op=ALU.add),
                     reads=[X[1], TN[1]], writes=[X[1]])
            outs.append(S.dma("sp", xov[:, blk * 4:(blk + 1) * 4, :], X[0][:], f"xo{b2}", reads=[X[1]]))
        S.wait_all("sp", outs)
        S.emit()
    return nc


_PROGS = {}


def _prog(name, fn, *args):
    key = (name,) + args
    if key not in _PROGS:
        _PROGS[key] = fn(*args)
    return _PROGS[key]


def _run(nc, in_maps):
    res = run_bass_kernel_spmd(nc, in_maps, core_ids=list(range(NCORES)))
    return res.results


def kernel(x, pre_norm_g, post_norm_g, w_in, ssm_a_re, ssm_a_im, ssm_log_dt, ssm_b_re, ssm_b_im,
           ssm_c_re, ssm_c_im, ssm_d, w_glu, b_glu, w_branch_ssm, w_branch_attn, w_out):
    f32 = lambda a: np.ascontiguousarray(np.asarray(a, dtype=np.float32))
    x = f32(x)
    NT = BATCH * SEQ // NCORES
    QPB = SEQ // NT
    xs = [np.ascontiguousarray(x.reshape(-1, D_MODEL)[r * NT:(r + 1) * NT]) for r in range(NCORES)]
    consts = attn_consts()
    bc = lambda v: np.ascontiguousarray(np.broadcast_to(f32(v), (128, D_MODEL)))
    for l in range(DEPTH):
        pre = _run(_prog("pre", build_pre, NT),
                   [dict(x=xs[r], gpre=bc(pre_norm_g[l]), w_in=f32(w_in[l]), ident=consts["ident"]) for r in range(NCORES)])
        cat = lambda key, b, ax: np.concatenate([pre[b * QPB + q][key] for q in range(QPB)], axis=ax)
        attn_in, ssm_in = [], []
        for b in range(BATCH):
            qf, kf, vf, uf = cat("qT", b, 1), cat("kT", b, 1), cat("v", b, 0), cat("uT", b, 1)
            for j in range(4):
                rows = slice(128 * j, 128 * j + 128)
                qT = np.zeros((2, 128, SEQ), NPBF)
                qT[0, 0:64] = qf[128 * j:128 * j + 64]
                qT[1, 64:128] = qf[128 * j + 64:128 * j + 128]
                m = dict(qT=qT, kTr=np.ascontiguousarray(kf[rows][:, ::-1]), vr=np.ascontiguousarray(vf[:, rows][::-1]))
                m.update(consts)
                attn_in.append(m)
                m2 = dict(uT=np.ascontiguousarray(uf[rows]))
                m2.update(ssm_layout(l, j, f32(ssm_a_re), f32(ssm_a_im), f32(ssm_log_dt), f32(ssm_b_re), f32(ssm_b_im),
                                     f32(ssm_c_re), f32(ssm_c_im), f32(ssm_d)))
                ssm_in.append(m2)
        ya = _run(_prog("attn", build_attn, SEQ), attn_in)
        ys = _run(_prog("ssm", build_ssm, SEQ), ssm_in)
        post_in = []
        for r in range(NCORES):
            b, q = divmod(r, QPB)
            tsl = slice(q * NT, (q + 1) * NT)
            yaT = np.ascontiguousarray(np.concatenate([ya[b * 4 + j]["yT"][:, tsl] for j in range(4)], axis=0))
            ysT = np.ascontiguousarray(np.concatenate([ys[b * 4 + j]["yT"][:, tsl] for j in range(4)], axis=0))
            post_in.append(dict(x=xs[r], gpost=bc(post_norm_g[l]), ysT=ysT, yaT=yaT,
                                szs=pre[r]["szs"], sza=pre[r]["sza"], sgs=pre[r]["sgs"], sga=pre[r]["sga"],
                                w_glu=f32(w_glu[l]), b_glu=np.ascontiguousarray(f32(b_glu[l]).reshape(8, 128).T),
                                w_bs=f32(w_branch_ssm[l]), w_ba=f32(w_branch_attn[l]), w_out=f32(w_out[l])))
        post = _run(_prog("post", build_post, NT), post_in)
        xs = [post[r]["xo"] for r in range(NCORES)]
    return np.concatenate(xs, axis=0).reshape(BATCH, SEQ, D_MODEL).astype(np.float32)
```
